# Optimizing a Trainium2 kernel written in Bass

```python
import jax, jax.numpy as jnp
from jax import lax
import numpy as np

D_MODEL = 2048
BATCH = 2
SEQ = 16384
DEPTH = 1

CHUNK = 64

D_MIX = D_MODEL
D_LRU = D_MIX // 2
D_POOL = D_MIX - D_LRU
LRU_HEADS = 16
LRU_HEAD_DIM = D_LRU // LRU_HEADS
CONV_WIDTH = 4
LRU_C = 8.0
POOL_WINDOWS = (2, 4, 8, 16)
POOL_GROUPS = len(POOL_WINDOWS)
POOL_GROUP_DIM = D_POOL // POOL_GROUPS
D_FF = ((8 * D_MODEL + 3 * 256 - 1) // (3 * 256)) * 256
N_MOD = 6
EPS = 1e-6

kernel_name = "hybrid_rglru_pool_swiglu_adaln"


def rmsnorm(x, g):
    xf = x.astype(jnp.float32)
    y = xf * lax.rsqrt(jnp.mean(xf * xf, axis=-1, keepdims=True) + EPS)
    return (y * g.astype(jnp.float32)).astype(x.dtype)


def modulate(h, shift, scale):
    return h * (1 + scale[:, None, :]) + shift[:, None, :]


def causal_dwconv(x, w, b):
    y = lax.conv_general_dilated(
        x, w[:, None, :].astype(x.dtype), window_strides=(1,),
        padding=[(CONV_WIDTH - 1, 0)],
        dimension_numbers=("NWC", "WIO", "NWC"),
        feature_group_count=x.shape[-1])
    return y + b.astype(x.dtype)


def rg_lru(x, w_a, b_a, w_i, b_i, lam):
    bsz, seq, _ = x.shape
    xf = x.astype(jnp.float32)
    xh = xf.reshape(bsz, seq, LRU_HEADS, LRU_HEAD_DIM)
    r = jax.nn.sigmoid(jnp.einsum("bshi,hij->bshj", xh, w_a.astype(jnp.float32)).reshape(bsz, seq, D_LRU)
                       + b_a.astype(jnp.float32))
    i = jax.nn.sigmoid(jnp.einsum("bshi,hij->bshj", xh, w_i.astype(jnp.float32)).reshape(bsz, seq, D_LRU)
                       + b_i.astype(jnp.float32))
    log_a = LRU_C * r * jax.nn.log_sigmoid(lam.astype(jnp.float32))
    a = jnp.exp(log_a)
    mult = jnp.sqrt(-jnp.expm1(2.0 * log_a))
    u = mult * (i * xf)

    def combine(left, right):
        a1, b1 = left
        a2, b2 = right
        return a1 * a2, a2 * b1 + b2

    _, h = lax.associative_scan(combine, (a, u), axis=1)
    return h


def pool_mixer(x, w_pool, ls_pool):
    bsz, seq, _ = x.shape
    xf = x.astype(jnp.float32)
    cs0 = jnp.concatenate([jnp.zeros((bsz, 1, D_POOL), jnp.float32), jnp.cumsum(xf, axis=1)], axis=1)
    pos1 = jnp.arange(1, seq + 1, dtype=jnp.float32)
    outs = []
    for g, w in enumerate(POOL_WINDOWS):
        sl = slice(g * POOL_GROUP_DIM, (g + 1) * POOL_GROUP_DIM)
        c0 = cs0[..., sl]
        upper = c0[:, 1:]
        lower = jnp.concatenate([jnp.zeros((bsz, w - 1, POOL_GROUP_DIM), jnp.float32),
                                 c0[:, :seq - w + 1]], axis=1)
        count = jnp.minimum(pos1, float(w))[None, :, None]
        outs.append((upper - lower) / count - xf[..., sl])
    pooled = jnp.stack(outs, axis=2)
    y = jnp.einsum("bsgc,gcd->bsgd", pooled, w_pool.astype(jnp.float32)) * ls_pool.astype(jnp.float32)
    return y.reshape(bsz, seq, D_POOL)


def setup_inputs(seed: int = 0) -> dict:
    key = jax.random.key(seed)
    ks = jax.random.split(key, 24)
    f32 = jnp.float32
    nrm = lambda k, shape, s: jax.random.normal(k, shape, f32) * s
    a0 = jax.random.uniform(ks[12], (DEPTH, D_LRU), f32, 0.9, 0.999)
    s0 = a0 ** (1.0 / LRU_C)
    lam = jnp.log(s0) - jnp.log1p(-s0)
    return {
        "x": nrm(ks[0], (BATCH, SEQ, D_MODEL), 1.0),
        "c": nrm(ks[1], (BATCH, D_MODEL), 1.0),
        "w_ada": nrm(ks[2], (DEPTH, D_MODEL, N_MOD * D_MODEL), 0.5 * D_MODEL ** -0.5),
        "b_ada": nrm(ks[3], (DEPTH, N_MOD * D_MODEL), 0.02),
        "g_norm_mix": 1.0 + nrm(ks[4], (DEPTH, D_MODEL), 0.05),
        "w_in": nrm(ks[5], (DEPTH, D_MODEL, 2 * D_LRU + D_POOL), D_MODEL ** -0.5),
        "w_conv": nrm(ks[6], (DEPTH, CONV_WIDTH, D_LRU), CONV_WIDTH ** -0.5),
        "b_conv": nrm(ks[7], (DEPTH, D_LRU), 0.02),
        "w_rg_a": nrm(ks[8], (DEPTH, LRU_HEADS, LRU_HEAD_DIM, LRU_HEAD_DIM), LRU_HEAD_DIM ** -0.5),
        "b_rg_a": nrm(ks[9], (DEPTH, D_LRU), 0.02),
        "w_rg_i": nrm(ks[10], (DEPTH, LRU_HEADS, LRU_HEAD_DIM, LRU_HEAD_DIM), LRU_HEAD_DIM ** -0.5),
        "b_rg_i": nrm(ks[11], (DEPTH, D_LRU), 0.02),
        "lru_lambda": lam,
        "w_pool": nrm(ks[13], (DEPTH, POOL_GROUPS, POOL_GROUP_DIM, POOL_GROUP_DIM), POOL_GROUP_DIM ** -0.5),
        "ls_pool": 1.0 + nrm(ks[14], (DEPTH, POOL_GROUPS, POOL_GROUP_DIM), 0.1),
        "w_out": nrm(ks[15], (DEPTH, D_MIX, D_MODEL), D_MIX ** -0.5),
        "g_norm_ffn": 1.0 + nrm(ks[16], (DEPTH, D_MODEL), 0.05),
        "w_ffn_gate": nrm(ks[17], (DEPTH, D_MODEL, D_FF), D_MODEL ** -0.5),
        "w_ffn_up": nrm(ks[18], (DEPTH, D_MODEL, D_FF), D_MODEL ** -0.5),
        "w_ffn_down": nrm(ks[19], (DEPTH, D_FF, D_MODEL), D_FF ** -0.5),
        "g_norm_final": 1.0 + nrm(ks[20], (D_MODEL,), 0.05),
    }


def reference(x, c, w_ada, b_ada, g_norm_mix, w_in, w_conv, b_conv, w_rg_a, b_rg_a,
              w_rg_i, b_rg_i, lru_lambda, w_pool, ls_pool, w_out, g_norm_ffn,
              w_ffn_gate, w_ffn_up, w_ffn_down, g_norm_final):
    dt = x.dtype
    c_act = jax.nn.silu(c)
    for l in range(DEPTH):
        mod = c_act @ w_ada[l] + b_ada[l]
        sh1, sc1, gt1, sh2, sc2, gt2 = jnp.split(mod, N_MOD, axis=-1)

        h = modulate(rmsnorm(x, g_norm_mix[l]), sh1, sc1)
        proj = h @ w_in[l]
        xr = proj[..., :D_LRU]
        gr = proj[..., D_LRU:2 * D_LRU]
        xp = proj[..., 2 * D_LRU:]
        xr = causal_dwconv(xr, w_conv[l], b_conv[l])
        hr = rg_lru(xr, w_rg_a[l], b_rg_a[l], w_rg_i[l], b_rg_i[l], lru_lambda[l])
        y_lru = (hr * jax.nn.gelu(gr.astype(jnp.float32))).astype(dt)
        y_pool = pool_mixer(xp, w_pool[l], ls_pool[l]).astype(dt)
        y = jnp.concatenate([y_lru, y_pool], axis=-1) @ w_out[l]
        x = x + gt1[:, None, :] * y

        h = modulate(rmsnorm(x, g_norm_ffn[l]), sh2, sc2)
        f = (jax.nn.silu(h @ w_ffn_gate[l]) * (h @ w_ffn_up[l])) @ w_ffn_down[l]
        x = x + gt2[:, None, :] * f
    return rmsnorm(x, g_norm_final)
```

```python
from contextlib import ExitStack
import numpy as np
import concourse.bass as bass
import concourse.mybir as mybir
from concourse.bass_utils import run_bass_kernel_spmd

F32 = mybir.dt.float32
BF16 = mybir.dt.bfloat16
AF = mybir.ActivationFunctionType
ALU = mybir.AluOpType

D = 2048
NKC = 16
T = 512
NT = 8
DFF = 5632
NJ = 44
EPS = 1e-6
NCORE = 8


class Trk:
    __slots__ = ("w", "r")

    def __init__(self):
        self.w = None
        self.r = []


class Buf:
    def __init__(self, ap, trks=None):
        self.ap = ap
        self.trks = trks if trks is not None else [Trk()]

    def v(self, ap):
        return Buf(ap, self.trks)


class Sched:
    ENG = ("pe", "act", "dve", "pool", "sp")

    def __init__(self, nc, es):
        self.nc = nc
        self.es = es
        self.q = {e: [] for e in self.ENG}
        self.cnt = {e: 0 for e in self.ENG}
        self.sem = {e: es.enter_context(nc.semaphore("sem_" + e)) for e in self.ENG}
        self.waited = {e: {} for e in self.ENG}
        self.semval = {}

    def new_sem(self, name):
        s = self.es.enter_context(self.nc.semaphore(name))
        self.semval[name] = 0
        return (name, s)

    def _deps(self, eng, reads, writes):
        deps = {}

        def add(d):
            if d is None:
                return
            k, s, v = d
            if k == "pe" and eng == "pe":
                return
            if deps.get(k, (None, 0))[1] < v:
                deps[k] = (s, v)

        for b in reads:
            for t in b.trks:
                add(t.w)
        for b in writes:
            for t in b.trks:
                add(t.w)
                for r in t.r:
                    add(r)
        for k, (s, v) in deps.items():
            if self.waited[eng].get(k, 0) < v:
                self.waited[eng][k] = v
                self.q[eng].append(("wait", s, v))

    def _mark(self, dep, reads, writes):
        for b in reads:
            for t in b.trks:
                t.r.append(dep)
        for b in writes:
            for t in b.trks:
                t.w = dep
                t.r = []

    def op(self, eng, fn, reads=(), writes=()):
        self._deps(eng, reads, writes)
        self.cnt[eng] += 1
        self.q[eng].append(("inst", fn, self.sem[eng], 1))
        self._mark((eng, self.sem[eng], self.cnt[eng]), reads, writes)

    def dma(self, eng, out_ap, in_ap, semh, reads=(), writes=(), serialize=False):
        self.dma_batch(eng, [(out_ap, in_ap, reads, writes)], semh, serialize)

    def dma_batch(self, eng, items, semh, serialize=True):
        name, s = semh
        if serialize and self.semval[name] > 0:
            self.q[eng].append(("wait", s, self.semval[name]))
        total = self.semval[name] + 16 * len(items)
        for out_ap, in_ap, reads, writes in items:
            self._deps(eng, reads, writes)
            self.q[eng].append(("inst", (lambda o, i_: (lambda e: e.dma_start(out=o, in_=i_)))(out_ap, in_ap), s, 16))
        for out_ap, in_ap, reads, writes in items:
            self._mark((name, s, total), reads, writes)
        self.semval[name] = total

    def wait_sem(self, eng, semh):
        name, s = semh
        self.q[eng].append(("wait", s, self.semval[name]))

    def emit(self, eng, e):
        for item in self.q[eng]:
            if item[0] == "wait":
                e.wait_ge(item[1], item[2])
            else:
                _, fn, s, n = item
                fn(e).then_inc(s, n)


def build_nc():
    nc = bass.Bass("TRN2", target_bir_lowering=False)
    dt = nc.dram_tensor
    x_d = dt("x", [NT, 128, NKC * T], F32, kind="ExternalInput").ap()
    xh_d = dt("xh", [128, NKC * 16], F32, kind="ExternalInput").ap()
    cT_d = dt("cT", [128, NKC * 2], F32, kind="ExternalInput").ap()
    wada_d = dt("wada", [D, 1536], F32, kind="ExternalInput").ap()
    bada_d = dt("bada", [128, 12], F32, kind="ExternalInput").ap()
    gn_d = dt("gn", [128, 48], F32, kind="ExternalInput").ap()
    win_d = dt("w_in", [D, 3072], F32, kind="ExternalInput").ap()
    wout_d = dt("w_out", [D, D], F32, kind="ExternalInput").ap()
    wg_d = dt("w_gate", [D, DFF], F32, kind="ExternalInput").ap()
    wu_d = dt("w_up", [D, DFF], F32, kind="ExternalInput").ap()
    wd_d = dt("w_down", [DFF, D], F32, kind="ExternalInput").ap()
    lruc_d = dt("lruc", [128, 64], F32, kind="ExternalInput").ap()
    wgate_d = dt("wgate", [128, 16 * 128], F32, kind="ExternalInput").ap()
    wpool_d = dt("wpool", [128, 8 * 256], F32, kind="ExternalInput").ap()
    lsp_d = dt("lsp", [128, 8], F32, kind="ExternalInput").ap()
    invc_d = dt("invc", [128, 64], F32, kind="ExternalInput").ap()
    msk_d = dt("msk", [128, 12], F32, kind="ExternalInput").ap()
    out_d = dt("out", [NT, 128, NKC * T], F32, kind="ExternalOutput").ap()

    ws_in = dt("ws_in", [6, 128, 16 * 512], BF16).ap()
    ws_out = dt("ws_out", [4, 128, 16 * 512], BF16).ap()
    ws_g = dt("ws_g", [11, 128, 16 * 512], BF16).ap()
    ws_u = dt("ws_u", [11, 128, 16 * 512], BF16).ap()
    ws_d = dt("ws_d", [16, 128, 11 * 512], BF16).ap()
    ag1i = dt("ag1i", [128, 24], F32).ap()
    ag1o = dt("ag1o", [NCORE * 128, 24], F32).ap()
    ag2i = dt("ag2i", [128, 16], F32).ap()
    ag2o = dt("ag2o", [NCORE * 128, 16], F32).ap()

    es = ExitStack()
    with es:
        S = Sched(nc, es)

        def sb(name, shape, dtype):
            return es.enter_context(nc.sbuf_tensor(name, shape, dtype))

        xt = [sb("xt%d" % i, [128, NKC * T], F32) for i in range(2)]
        xb = [[Buf(xt[i][:, k * T:(k + 1) * T]) for k in range(NKC)] for i in range(2)]
        xall = [Buf(xt[i][:], [b.trks[0] for b in xb[i]]) for i in range(2)]
        ht = sb("ht", [128, NKC * T], BF16)
        hb = [Buf(ht[:, k * T:(k + 1) * T]) for k in range(NKC)]
        NSQ = 5
        sqt = [Buf(sb("sq%d" % i, [128, T], BF16)[:]) for i in range(NSQ)]
        tbt = [Buf(sb("tb%d" % i, [128, T], F32)[:]) for i in range(2)]
        rstdA = Buf(sb("rstdA", [128, T], F32)[:])
        rstdB = Buf(sb("rstdB", [128, T], F32)[:])
        wt = [sb("wslot%d" % i, [128, 16 * 512], BF16) for i in range(2)]
        wslot = [Buf(wt[i][:]) for i in range(2)]
        UE = 33536
        ut = sb("U", [128, UE], BF16)
        PG = 512
        upages = [Trk() for _ in range((UE + PG - 1) // PG)]

        def ubuf(off, n, dtype=BF16):
            ap = ut[:, off:off + n]
            if dtype == F32:
                ap = ap.bitcast(F32)
            return Buf(ap, upages[off // PG:(off + n - 1) // PG + 1])

        act = [ubuf(j * T, T) for j in range(NJ)]
        sgt = [ubuf(NJ * T + i * 2 * T, 2 * T, F32) for i in range(2)]
        NSET = 7
        o = 0
        LS = []
        for s in range(NSET):
            d_ = {}
            d_["XR"] = ubuf(o, 2 * 516, F32); o += 2 * 516
            d_["XCB"] = ubuf(o, T); o += T
            d_["TI"] = ubuf(o, 2 * T, F32); o += 2 * T
            d_["XC"] = ubuf(o, 2 * T, F32); o += 2 * T
            d_["M"] = ubuf(o, 2 * T, F32); o += 2 * T
            d_["PM"] = ubuf(o - 4 * T, 4 * T, F32)
            LS.append(d_)
        assert o <= UE, o
        o = 0
        ybuf = []
        for k in range(16):
            ybuf.append(ubuf(o, T)); o += T
        XP = []
        for s in range(2):
            XP.append(ubuf(o, 2 * 528, F32)); o += 2 * 528
        PA = ubuf(o, 2 * 528, F32); o += 2 * 528
        PB_ = ubuf(o, 2 * 528, F32); o += 2 * 528
        PBF = []
        for s in range(4):
            PBF.append(ubuf(o, T)); o += T
        Gt = []
        for s in range(2):
            Gt.append(ubuf(o, 2 * T, F32)); o += 2 * T
        LPG = []
        for s in range(2):
            LPG.append(ubuf(o, 16 * T, F32)); o += 16 * T
        assert o <= UE, o
        wada_st = ubuf(0, 2 * 16 * 768, F32)

        cst = sb("cst", [128, 1056], F32)
        co = [0]

        def cbuf(n):
            b = Buf(cst[:, co[0]:co[0] + n])
            co[0] += n
            return b

        cTb = cbuf(32); siluT = cbuf(32); badab = cbuf(12); gnb = cbuf(48)
        lrucb = cbuf(64); lspb = cbuf(8); invcb = cbuf(64); mskb = cbuf(12)
        modpart = cbuf(24); modall = cbuf(192); modsel = cbuf(96); modtmp = cbuf(96)
        geff = cbuf(32); hcl = cbuf(8); hba = cbuf(8); hbi = cbuf(8); ltmp = cbuf(8)
        state = cbuf(8); pstate = cbuf(8); ag2sb = cbuf(128); xrh = cbuf(24); xph = cbuf(120)
        cA = cbuf(8); cH = cbuf(8)
        assert co[0] <= 1056, co[0]
        zeros = Buf(sb("zeros", [128, T], F32)[:])
        ones_bf = Buf(sb("ones", [128, 128], BF16)[:])
        wgb = Buf(sb("wgb", [128, 16 * 128], BF16)[:])
        wpb = Buf(sb("wpb", [128, 8 * 256], BF16)[:])
        xhb = Buf(sb("xhb", [128, NKC * 16], F32)[:])
        hhb = Buf(sb("hhb", [128, NKC * 16], BF16)[:])

        banks = [Buf(es.enter_context(nc.psum_tensor("ps%d" % i, [128, T], F32))[:]) for i in range(8)]
        bk = [0]

        def bank():
            b = banks[bk[0] % 6]
            bk[0] += 1
            return b

        s_misc = S.new_sem("s_misc")
        s_const = s_misc
        s_xl = S.new_sem("s_x")
        s_x = [s_xl, s_xl]
        s_ol = S.new_sem("s_o")
        s_o = [s_ol, s_ol]
        s_w = [S.new_sem("s_w0"), S.new_sem("s_w1")]
        s_cv = [S.new_sem("s_cv0"), S.new_sem("s_cv1")]
        s_ph = S.new_sem("s_ph")
        s_lpq = S.new_sem("s_lpq")

        def A(eng, fn, reads, writes):
            S.op(eng, fn, reads, writes)

        def act_fn(out, in_, func, reads, writes, bias=None, scale=None):
            kw = {}
            if bias is not None:
                kw["bias"] = bias
            if scale is not None:
                kw["scale"] = scale
            A("act", lambda e: e.activation(out=out, in_=in_, func=func, **kw), reads, writes)

        def tt(out, in0, in1, op, reads, writes):
            A("dve", lambda e: e.tensor_tensor(out=out, in0=in0, in1=in1, op=op), reads, writes)

        def ts(out, in0, s1, s2, op0, op1, reads, writes):
            if s2 is None:
                A("dve", lambda e: e.tensor_scalar(out=out, in0=in0, scalar1=s1, scalar2=None, op0=op0), reads, writes)
            else:
                A("dve", lambda e: e.tensor_scalar(out=out, in0=in0, scalar1=s1, scalar2=s2, op0=op0, op1=op1), reads, writes)

        def stt(out, in0, sc, in1, op0, op1, reads, writes):
            A("dve", lambda e: e.scalar_tensor_tensor(out=out, in0=in0, scalar=sc, in1=in1, op0=op0, op1=op1), reads, writes)

        def cp(out, in_, reads, writes):
            A("dve", lambda e: e.tensor_copy(out=out, in_=in_), reads, writes)

        def mm_group(outb, pairs, reads, start=True, stop=True, out_ap=None, fresh_check=True):
            oap = out_ap if out_ap is not None else outb.ap
            n = len(pairs)
            if start and fresh_check:
                for t in outb.trks:
                    assert t.w is None or t.w[0] != "pe" or len(t.r) > 0, "PSUM bank reused before it was read"

            def fn(e):
                last = None
                for i, (l, r) in enumerate(pairs):
                    last = e.matmul(oap, l, r, start=(start and i == 0), stop=(stop and i == n - 1))
                return last
            A("pe", fn, reads, [outb])

        cl = [(cTb, cT_d), (badab, bada_d), (gnb, gn_d), (lrucb, lruc_d), (lspb, lsp_d),
              (invcb, invc_d), (mskb, msk_d), (xhb, xh_d)]
        stg1 = Buf(xt[1][:, 0:2048], [t for b in xb[1][0:4] for t in b.trks])
        stg2 = Buf(xt[1][:, 2048:4096], [t for b in xb[1][4:8] for t in b.trks])
        items = [(b.ap, d_[:, :], [], [b]) for b, d_ in cl]
        items.append((stg1.ap, wgate_d[:, :], [], [stg1]))
        items.append((stg2.ap, wpool_d[:, :], [], [stg2]))
        S.dma_batch("sp", items, s_const)
        cp(wgb.ap, stg1.ap, [stg1], [wgb])
        cp(wpb.ap, stg2.ap, [stg2], [wpb])
        A("dve", lambda e: e.memset(zeros.ap, 0.0), [], [zeros])
        A("dve", lambda e: e.memset(ones_bf.ap, 1.0), [], [ones_bf])
        A("dve", lambda e: e.memset(state.ap, 0.0), [], [state])
        A("dve", lambda e: e.memset(pstate.ap, 1.0), [], [pstate])

        lc3 = lrucb.ap.rearrange("p (c k) -> p c k", k=8)
        act_fn(ltmp.ap, lc3[:, :, 7], AF.Exp, [lrucb], [ltmp], scale=-1.0)
        act_fn(ltmp.ap, ltmp.ap, AF.Ln, [ltmp], [ltmp], bias=1.0)
        ts(hcl.ap, ltmp.ap, -4.0, None, ALU.mult, None, [ltmp], [hcl])
        ts(hba.ap, lc3[:, :, 5], 0.5, None, ALU.mult, None, [lrucb], [hba])
        ts(hbi.ap, lc3[:, :, 6], 0.5, None, ALU.mult, None, [lrucb], [hbi])

        act_fn(siluT.ap, cTb.ap, AF.Silu, [cTb], [siluT])
        pm = bank()
        for half in range(2):
            S.dma("sp", wada_st.ap.rearrange("p (k j) -> p k j", k=16),
                  wada_d[:, half * 768:(half + 1) * 768].rearrange("(k p) j -> p k j", p=128),
                  s_misc, [], [wada_st], serialize=True)
            w3 = wada_st.ap.rearrange("p (k j) -> p k j", k=16)
            for jj in range(6):
                col = (half * 6 + jj) * 2
                pairs = [(w3[:, k, jj * 128:(jj + 1) * 128], siluT.ap[:, 2 * k:2 * k + 2]) for k in range(16)]
                mm_group(pm, pairs, [wada_st, siluT], out_ap=pm.ap[:, col:col + 2], fresh_check=False)
        pm3 = pm.ap[:, 0:24].rearrange("p (j b) -> p j b", b=2)
        mp3 = modpart.ap.rearrange("p (j b) -> p j b", b=2)
        for b_ in range(2):
            tt(mp3[:, :, b_], pm3[:, :, b_], badab.ap, ALU.add, [pm, badab], [modpart])
        ag1i_b = Buf(ag1i); ag1o_b = Buf(ag1o); ag2i_b = Buf(ag2i); ag2o_b = Buf(ag2o)
        S.dma("sp", ag1i[:, :], modpart.ap, s_misc, [modpart], [ag1i_b], serialize=True)

        conv_list = []
        scr_in = {}; scr_out = {}; scr_g = {}; scr_u = {}; scr_d = {}
        for s in (0, 1, 4, 5, 2, 3):
            conv_list.append((ws_in[s], win_d[:, s * 512:(s + 1) * 512], 16, scr_in, s))
        for s in range(4):
            conv_list.append((ws_out[s], wout_d[:, s * 512:(s + 1) * 512], 16, scr_out, s))
        for s in range(11):
            conv_list.append((ws_g[s], wg_d[:, s * 512:(s + 1) * 512], 16, scr_g, s))
            conv_list.append((ws_u[s], wu_d[:, s * 512:(s + 1) * 512], 16, scr_u, s))
        for mg in range(4):
            for kg in range(4):
                conv_list.append((ws_d[mg * 4 + kg],
                                  wd_d[kg * 11 * 128:(kg + 1) * 11 * 128, mg * 512:(mg + 1) * 512], 11,
                                  scr_d, mg * 4 + kg))
        for dst, src, nk, dct, key in conv_list:
            dct[key] = Buf(dst)
        cvi = [0]

        def do_conv(n):
            for _ in range(n):
                if cvi[0] >= len(conv_list):
                    return
                i = cvi[0]
                dst, src, nk, dct, key = conv_list[i]
                b = dct[key]
                semh = s_cv[i % 2]
                if i >= 1:
                    prev = s_cv[(i - 1) % 2]
                    S.q["pool"].append(("wait", prev[1], S.semval[prev[0]]))
                S.dma("pool", dst.rearrange("p (k j) -> p k j", k=nk),
                      src.rearrange("(k p) j -> p k j", p=128), semh, [], [b])
                cvi[0] += 1

        do_conv(4)
        A("pool", lambda e: e.collective_compute("AllGather", ALU.bypass, replica_groups=[list(range(NCORE))],
                                                 ins=[ag1i[:, :]], outs=[ag1o[:, :]]), [ag1i_b], [ag1o_b])
        S.dma("sp", modall.ap.rearrange("p (r c) -> p r c", r=NCORE),
              ag1o.rearrange("(r p) c -> p r c", p=128), s_misc, [ag1o_b], [modall], serialize=True)
        ma3 = modall.ap.rearrange("p (n b) -> p n b", b=2)
        ts(modtmp.ap, ma3[:, :, 0], mskb.ap[:, 9:10], None, ALU.mult, None, [modall, mskb], [modtmp])
        stt(modsel.ap, ma3[:, :, 1], mskb.ap[:, 10:11], modtmp.ap, ALU.mult, ALU.add, [modall, mskb, modtmp], [modsel])
        SH1, SC1, GT1, SH2, SC2, GT2 = [modsel.ap[:, i * 16:(i + 1) * 16] for i in range(6)]
        stt(geff.ap[:, 0:16], SC1, 1.0, gnb.ap[:, 0:16], ALU.add, ALU.mult, [modsel, gnb], [geff])
        stt(geff.ap[:, 16:32], SC2, 1.0, gnb.ap[:, 16:32], ALU.add, ALU.mult, [modsel, gnb], [geff])
        do_conv(16)

        def load_x(i, par):
            S.dma("sp", xt[par][:], x_d[i], s_x[par], [], [xall[par]], serialize=True)

        class NormAcc:
            def __init__(self, bankb, bufs, width=T, lag=0):
                assert lag < len(bufs)
                self.b = bankb
                self.bufs = bufs
                self.n = 0
                self.k = 0
                self.w = width
                self.lag = lag
                self.pend = []

            def _mm(self):
                w = self.w
                sq = self.pend.pop(0)
                mm_group(self.b, [(ones_bf.ap, sq.ap[:, :w])], [ones_bf, sq],
                         start=(self.k == 0), stop=(self.k == NKC - 1), out_ap=self.b.ap[:, :w])
                self.k += 1

            def add(self, xk):
                w = self.w
                assert len(self.pend) < len(self.bufs)
                sq = self.bufs[self.n % len(self.bufs)]
                self.n += 1
                xin = xk.ap[:, :w] if w == T else xk.ap
                act_fn(sq.ap[:, :w], xin, AF.Square, [xk], [sq])
                self.pend.append(sq)
                while len(self.pend) > self.lag:
                    self._mm()

            def finish(self, rs):
                w = self.w
                while self.pend:
                    self._mm()
                assert self.k == NKC
                act_fn(rs.ap[:, :w], self.b.ap[:, :w], AF.Sqrt, [self.b], [rs], bias=EPS, scale=1.0 / D)
                A("dve", lambda e: e.reciprocal(out=rs.ap[:, :w], in_=rs.ap[:, :w]), [rs], [rs])

        def norm_stats(xs, rs, bankb, width=T):
            na = NormAcc(bankb, sqt[3:5], width, lag=1)
            for k in range(NKC):
                na.add(xs[k])
            na.finish(rs)

        def norm_mod(xs, gcol, shap, hs, rs, width=T):
            for k in range(NKC):
                tb = tbt[k % 2]
                xin = xs[k].ap[:, :width] if width == T else xs[k].ap
                tt(tb.ap[:, :width], xin, rs.ap[:, :width], ALU.mult, [xs[k], rs], [tb])
                act_fn(hs[k].ap if width != T else hs[k].ap[:, :width], tb.ap[:, :width], AF.Identity,
                       [tb, geff, modsel], [hs[k]], bias=shap[:, k:k + 1], scale=geff.ap[:, gcol + k:gcol + k + 1])

        wq = {"n": 0}

        def load_slab(src_buf, nk=16):
            i = wq["n"] % 2
            wq["n"] += 1
            S.dma("sp", wt[i][:, 0:nk * 512], src_buf.ap, s_w[i], [src_buf], [wslot[i]])
            return wslot[i], wt[i][:, 0:nk * 512].rearrange("p (k j) -> p k j", k=nk)

        def proj_chunk(slot, w3, jj, rhs_list, nk=16):
            ps = bank()
            pairs = [(w3[:, k, jj * 128:(jj + 1) * 128], rhs_list[k].ap) for k in range(nk)]
            mm_group(ps, pairs, [slot] + list(rhs_list))
            return ps

        wc = lambda c, k: lrucb.ap[:, c * 8 + k:c * 8 + k + 1]

        outB = [Buf(out_d[i]) for i in range(NT)]
        lpB = [[outB[i].v(out_d[i][:, c * 2 * T:(c + 1) * 2 * T]) for c in range(8)] for i in range(NT)]

        def lru_conv(c, ps, st):
            XR, XC, XCB = st["XR"], st["XC"], st["XCB"]
            cp(XR.ap[:, 0:3], xrh.ap[:, c * 3:c * 3 + 3], [xrh], [XR])
            act_fn(XR.ap[:, 3:515], ps.ap, AF.Copy, [ps], [XR])
            cp(xrh.ap[:, c * 3:c * 3 + 3], XR.ap[:, 512:515], [XR], [xrh])
            act_fn(XC.ap, XR.ap[:, 3:515], AF.Identity, [XR, lrucb], [XC], bias=wc(c, 4), scale=wc(c, 3))
            for k in (2, 1, 0):
                stt(XC.ap, XR.ap[:, k:k + 512], wc(c, k), XC.ap, ALU.mult, ALU.add, [XR, XC, lrucb], [XC])
            act_fn(XCB.ap, XC.ap, AF.Copy, [XC], [XCB])

        def lru_gates(c, st):
            pa = bank()
            mm_group(pa, [(wgb.ap[:, c * 128:(c + 1) * 128], st["XCB"].ap)], [wgb, st["XCB"]])
            pi = bank()
            mm_group(pi, [(wgb.ap[:, (8 + c) * 128:(9 + c) * 128], st["XCB"].ap)], [wgb, st["XCB"]])
            TA = st["XR"].ap[:, 0:512]
            act_fn(TA, pa.ap, AF.Tanh, [pa, hba], [st["XR"]], bias=hba.ap[:, c:c + 1], scale=0.5)
            act_fn(st["TI"].ap, pi.ap, AF.Tanh, [pi, hbi], [st["TI"]], bias=hbi.ap[:, c:c + 1], scale=0.5)

        def lru_exp(c, st):
            TA = st["XR"].ap[:, 0:512]
            act_fn(TA, TA, AF.Exp, [st["XR"], hcl], [st["XR"]], bias=hcl.ap[:, c:c + 1], scale=hcl.ap[:, c:c + 1])
            act_fn(st["M"].ap, TA, AF.Square, [st["XR"]], [st["M"]])
            stt(st["TI"].ap, st["TI"].ap, 1.0, st["XC"].ap, ALU.add, ALU.mult, [st["TI"], st["XC"]], [st["TI"]])

        def lru_sqrt_scan(i, c, st, sidx):
            TA = st["XR"].ap[:, 0:512]
            act_fn(st["M"].ap, st["M"].ap, AF.Sqrt, [st["M"]], [st["M"]], bias=0.25, scale=-0.25)
            tt(st["TI"].ap, st["TI"].ap, st["M"].ap, ALU.mult, [st["TI"], st["M"]], [st["TI"]])
            A("dve", lambda e: e.tensor_tensor_scan(out=st["M"].ap, data0=TA, data1=st["TI"].ap,
                                                    initial=state.ap[:, c:c + 1], op0=ALU.mult, op1=ALU.add),
              [st["XR"], st["TI"], state], [st["M"]])
            cp(state.ap[:, c:c + 1], st["M"].ap[:, 511:512], [st["M"]], [state])
            A("dve", lambda e: e.tensor_tensor_scan(out=st["XC"].ap, data0=TA, data1=zeros.ap,
                                                    initial=pstate.ap[:, c:c + 1], op0=ALU.mult, op1=ALU.add),
              [st["XR"], zeros, pstate], [st["XC"]])
            cp(pstate.ap[:, c:c + 1], st["XC"].ap[:, 511:512], [st["XC"]], [pstate])

        setc = [0]
        grpc = [0]

        def lru_group(i, cs, slot_x, w3x, hs):
            sid = {}
            for c in cs:
                sid[c] = setc[0] % NSET
                setc[0] += 1
            sts = {c: LS[sid[c]] for c in cs}
            for c in cs:
                ps = proj_chunk(slot_x, w3x, c % 4, hs)
                lru_conv(c, ps, sts[c])
            for c in cs:
                lru_gates(c, sts[c])
            for c in cs:
                lru_exp(c, sts[c])
            for c in cs:
                lru_sqrt_scan(i, c, sts[c], sid[c])
            items = []
            for c in cs:
                items.append((lpB[i][c].ap, sts[c]["PM"].ap, [sts[c]["PM"]], [lpB[i][c]]))
            S.dma_batch("sp", items, s_ph)

        def pool_chunk(c, ps, first_tile):
            g = c // 2
            w = 2 << g
            X = XP[c % 2]
            cp(X.ap[:, 0:15], xph.ap[:, c * 15:c * 15 + 15], [xph], [X])
            act_fn(X.ap[:, 15:527], ps.ap, AF.Copy, [ps], [X])
            cp(xph.ap[:, c * 15:c * 15 + 15], X.ap[:, 512:527], [X], [xph])
            src = X
            off = 0
            dsts = [PA, PB_]
            for lv in range(g + 1):
                sh = 1 << lv
                dst = dsts[lv % 2]
                n0 = off + sh
                tt(dst.ap[:, n0:527], src.ap[:, n0:527], src.ap[:, n0 - sh:527 - sh], ALU.add, [src], [dst])
                src = dst
                off = n0
            pb = PBF[c % 4]
            stt(pb.ap, src.ap[:, 15:527], 1.0 / w, X.ap[:, 15:527], ALU.mult, ALU.subtract, [src, X], [pb])
            if first_tile:
                t16 = tbt[0]
                tt(t16.ap[:, 0:16], src.ap[:, 15:31], invcb.ap[:, g * 16:(g + 1) * 16], ALU.mult, [src, invcb], [t16])
                tt(pb.ap[:, 0:16], t16.ap[:, 0:16], X.ap[:, 15:31], ALU.subtract, [t16, X], [pb])
            return pb

        def resid_update(xs, m, ps, gcol):
            stt(xs[m].ap, ps.ap, modsel.ap[:, gcol + m:gcol + m + 1], xs[m].ap, ALU.mult, ALU.add,
                [ps, modsel, xs[m]], [xs[m]])

        def halo_proj():
            hx = [Buf(xhb.ap[:, k * 16:(k + 1) * 16], xhb.trks) for k in range(NKC)]
            hh = [Buf(hhb.ap[:, k * 16:(k + 1) * 16], hhb.trks) for k in range(NKC)]
            norm_stats(hx, rstdA, banks[6], width=16)
            norm_mod(hx, 0, SH1, hh, rstdA, width=16)
            res = {}
            for s in (4, 5, 0, 1):
                slot, w3 = load_slab(scr_in[s])
                res[s] = (slot, w3)
                for jj in range(4):
                    ps = bank()
                    pairs = [(w3[:, k, jj * 128:(jj + 1) * 128], hh[k].ap) for k in range(NKC)]
                    mm_group(ps, pairs, [slot, hhb], out_ap=ps.ap[:, 0:16])
                    c = s * 4 + jj if s < 2 else (s - 4) * 4 + jj
                    if s < 2:
                        ts(xrh.ap[:, c * 3:c * 3 + 3], ps.ap[:, 13:16], mskb.ap[:, 8:9], None, ALU.mult, None,
                           [ps, mskb], [xrh])
                    else:
                        ts(xph.ap[:, c * 15:c * 15 + 15], ps.ap[:, 1:16], mskb.ap[:, 8:9], None, ALU.mult, None,
                           [ps, mskb], [xph])
            return res

        load_x(0, 0)
        hres = halo_proj()
        slot0, w30 = hres[0]
        slot1, w31 = hres[1]
        for i in range(NT):
            par = i % 2
            if i + 1 < NT:
                load_x(i + 1, (i + 1) % 2)
            norm_stats(xb[par], rstdA, banks[6])
            norm_mod(xb[par], 0, SH1, hb, rstdA)
            lru_group(i, [0, 1, 2, 3], slot0, w30, hb)
            lru_group(i, [4, 5, 6, 7], slot1, w31, hb)
        load_x(0, 0)
        cp(ag2sb.ap[:, 0:8], pstate.ap, [pstate], [ag2sb])
        cp(ag2sb.ap[:, 8:16], state.ap, [state], [ag2sb])
        S.dma("sp", ag2i[:, :], ag2sb.ap[:, 0:16], s_misc, [ag2sb], [ag2i_b], serialize=True)
        A("pool", lambda e: e.collective_compute("AllGather", ALU.bypass, replica_groups=[list(range(NCORE))],
                                                 ins=[ag2i[:, :]], outs=[ag2o[:, :]]), [ag2i_b], [ag2o_b])
        do_conv(100)

        def combine_state():
            S.dma("sp", ag2sb.ap.rearrange("p (r c) -> p r c", r=NCORE),
                  ag2o.rearrange("(r p) c -> p r c", p=128), s_misc, [ag2o_b], [ag2sb], serialize=True)
            A("dve", lambda e: e.memset(state.ap, 0.0), [], [state])
            for r in range(NCORE):
                mj = mskb.ap[:, r:r + 1]
                Ar = ag2sb.ap[:, r * 16:r * 16 + 8]
                Hr = ag2sb.ap[:, r * 16 + 8:r * 16 + 16]
                ts(cA.ap, Ar, -1.0, None, ALU.add, None, [ag2sb], [cA])
                ts(cA.ap, cA.ap, mj, None, ALU.mult, None, [cA, mskb], [cA])
                ts(cA.ap, cA.ap, 1.0, None, ALU.add, None, [cA], [cA])
                ts(cH.ap, Hr, mj, None, ALU.mult, None, [ag2sb, mskb], [cH])
                tt(state.ap, state.ap, cA.ap, ALU.mult, [state, cA], [state])
                tt(state.ap, state.ap, cH.ap, ALU.add, [state, cH], [state])

        norm_stats(xb[0], rstdA, banks[6])
        norm_mod(xb[0], 0, SH1, hb, rstdA)
        for i in range(NT):
            par = i % 2
            npar = (i + 1) % 2
            xs = xb[par]
            first = (i == 0)
            for g in range(2):
                S.dma("pool", LPG[g].ap, out_d[i][:, g * 8 * T:(g + 1) * 8 * T], s_lpq, [outB[i]], [LPG[g]],
                      serialize=True)
            for s in (2, 3):
                sg_, wg_ = load_slab(scr_in[s])
                for jj in range(4):
                    c = (s - 2) * 4 + jj
                    lp = LPG[s - 2].v(LPG[s - 2].ap[:, jj * 2 * T:(jj + 1) * 2 * T])
                    if first and c == 0:
                        combine_state()
                    pg = proj_chunk(sg_, wg_, jj, hb)
                    G = Gt[c % 2]
                    act_fn(G.ap, pg.ap, AF.Gelu, [pg], [G])
                    stt(lp.ap[:, 0:T], lp.ap[:, 0:T], state.ap[:, c:c + 1], lp.ap[:, T:2 * T],
                        ALU.mult, ALU.add, [lp, state], [lp])
                    tt(ybuf[c].ap, lp.ap[:, 0:T], G.ap, ALU.mult, [lp, G], [ybuf[c]])
            for s in (4, 5):
                sp_, wp_ = load_slab(scr_in[s])
                for jj in range(4):
                    c = (s - 4) * 4 + jj
                    ps = proj_chunk(sp_, wp_, jj, hb)
                    pool_chunk(c, ps, first)
                    if c % 2 == 1:
                        g = c // 2
                        for dd in range(2):
                            po = bank()
                            pairs = [(wpb.ap[:, (g * 2 + kc) * 256 + dd * 128:(g * 2 + kc) * 256 + (dd + 1) * 128],
                                      PBF[(2 * g + kc) % 4].ap) for kc in range(2)]
                            mm_group(po, pairs, [wpb, PBF[(2 * g) % 4], PBF[(2 * g + 1) % 4]])
                            yb = ybuf[8 + 2 * g + dd]
                            act_fn(yb.ap, po.ap, AF.Identity, [po, lspb], [yb],
                                   scale=lspb.ap[:, 2 * g + dd:2 * g + dd + 1])
            na = NormAcc(banks[6], sqt[3:5], lag=1)
            for s in range(4):
                so, wo = load_slab(scr_out[s])
                for jj in range(4):
                    ps = proj_chunk(so, wo, jj, ybuf)
                    resid_update(xs, s * 4 + jj, ps, 32)
                    na.add(xs[s * 4 + jj])
            if i + 1 < NT:
                load_x(i + 1, npar)
            na.finish(rstdA)
            norm_mod(xs, 16, SH2, hb, rstdA)
            for s in range(11):
                sgs, wgs = load_slab(scr_g[s])
                sus, wus = load_slab(scr_u[s])
                for jj in range(4):
                    j = s * 4 + jj
                    pg = proj_chunk(sgs, wgs, jj, hb)
                    pu = proj_chunk(sus, wus, jj, hb)
                    sg = sgt[j % 2]
                    act_fn(sg.ap, pg.ap, AF.Silu, [pg], [sg])
                    tt(act[j].ap, sg.ap, pu.ap, ALU.mult, [sg, pu], [act[j]])
            nf = NormAcc(banks[7], sqt[0:3], lag=2)
            n1 = None
            for mg in range(4):
                pbs = [bank() for _ in range(4)]
                for kg in range(4):
                    sd, wd3 = load_slab(scr_d[mg * 4 + kg], nk=11)
                    for m in range(4):
                        pairs = [(wd3[:, k, m * 128:(m + 1) * 128], act[kg * 11 + k].ap) for k in range(11)]
                        mm_group(pbs[m], pairs, [sd] + act[kg * 11:(kg + 1) * 11], start=(kg == 0), stop=(kg == 3))
                        if mg == 1 and i + 1 < NT:
                            if n1 is None:
                                n1 = NormAcc(banks[6], sqt[3:5], lag=1)
                            n1.add(xb[npar][kg * 4 + m])
                for m in range(4):
                    resid_update(xs, mg * 4 + m, pbs[m], 80)
                    nf.add(xs[mg * 4 + m])
                if mg == 1 and i + 1 < NT:
                    n1.finish(rstdA)
                    norm_mod(xb[npar], 0, SH1, hb, rstdA)
            nf.finish(rstdB)
            for k in range(NKC):
                stt(xs[k].ap, xs[k].ap, gnb.ap[:, 32 + k:33 + k], rstdB.ap, ALU.mult, ALU.mult,
                    [xs[k], gnb, rstdB], [xs[k]])
            S.dma("sp", out_d[i], xt[par][:], s_o[par], [xall[par]], [outB[i]], serialize=True)
        for par in range(2):
            S.wait_sem("sp", s_o[par])

        with nc.Block() as block:
            @block.sync
            def _(e):
                S.emit("sp", e)

            @block.gpsimd
            def _(e):
                S.emit("pool", e)

            @block.scalar
            def _(e):
                S.emit("act", e)

            @block.vector
            def _(e):
                S.emit("dve", e)

            @block.tensor
            def _(e):
                S.emit("pe", e)
    return nc


def _host_inputs(x, c, w_ada, b_ada, g_norm_mix, w_in, w_conv, b_conv, w_rg_a, b_rg_a,
                 w_rg_i, b_rg_i, lru_lambda, w_pool, ls_pool, w_out, g_norm_ffn,
                 w_ffn_gate, w_ffn_up, w_ffn_down, g_norm_final):
    f = np.float32
    x = np.asarray(x, f); c = np.asarray(c, f)
    col = lambda v, n: np.ascontiguousarray(np.asarray(v, f).reshape(n, 128).T)
    cT = np.ascontiguousarray(c.reshape(2, 16, 128).transpose(2, 1, 0)).reshape(128, 32)
    gn = np.concatenate([col(g_norm_mix[0], 16), col(g_norm_ffn[0], 16), col(g_norm_final, 16)], axis=1)
    lruc = np.zeros((128, 8, 8), f)
    wcv = np.asarray(w_conv[0], f)
    for k in range(4):
        lruc[:, :, k] = col(wcv[k], 8)
    lruc[:, :, 4] = col(b_conv[0], 8)
    lruc[:, :, 5] = col(b_rg_a[0], 8)
    lruc[:, :, 6] = col(b_rg_i[0], 8)
    lruc[:, :, 7] = col(lru_lambda[0], 8)
    wgate = np.zeros((128, 16, 128), f)
    wa = np.asarray(w_rg_a[0], f); wi = np.asarray(w_rg_i[0], f)
    for cc in range(8):
        for hh in range(2):
            wgate[hh * 64:(hh + 1) * 64, cc, hh * 64:(hh + 1) * 64] = wa[2 * cc + hh]
            wgate[hh * 64:(hh + 1) * 64, 8 + cc, hh * 64:(hh + 1) * 64] = wi[2 * cc + hh]
    wp = np.asarray(w_pool[0], f)
    wpool = np.ascontiguousarray(wp.reshape(4, 2, 128, 256).transpose(2, 0, 1, 3)).reshape(128, 8 * 256)
    lsp = col(np.asarray(ls_pool[0], f).reshape(-1), 8)
    w_in0 = np.ascontiguousarray(np.asarray(w_in[0], f))
    w_out0 = np.ascontiguousarray(np.asarray(w_out[0], f))
    wg0 = np.ascontiguousarray(np.asarray(w_ffn_gate[0], f))
    wu0 = np.ascontiguousarray(np.asarray(w_ffn_up[0], f))
    wd0 = np.ascontiguousarray(np.asarray(w_ffn_down[0], f))
    wada0 = np.asarray(w_ada[0], f)
    bada0 = np.asarray(b_ada[0], f)
    maps = []
    for r in range(NCORE):
        b, q = r // 4, r % 4
        t0 = q * 4096
        xs = x[b, t0:t0 + 4096, :]
        xt_ = np.ascontiguousarray(xs.reshape(NT, T, NKC, 128).transpose(0, 3, 2, 1)).reshape(NT, 128, NKC * T)
        if q > 0:
            xh = x[b, t0 - 16:t0, :]
        else:
            xh = np.zeros((16, D), f)
        xh_ = np.ascontiguousarray(xh.reshape(16, NKC, 128).transpose(2, 1, 0)).reshape(128, NKC * 16)
        invc = np.zeros((128, 4, 16), f)
        for g in range(4):
            w = 2 << g
            pos1 = np.arange(t0 + 1, t0 + 17)
            invc[:, g, :] = (1.0 / np.minimum(pos1, w)).astype(f)[None, :]
        msk = np.zeros((128, 12), f)
        for j in range(NCORE):
            if j // 4 == b and j < r:
                msk[:, j] = 1.0
        msk[:, 8] = 1.0 if q > 0 else 0.0
        msk[:, 9] = 1.0 if b == 0 else 0.0
        msk[:, 10] = 1.0 if b == 1 else 0.0
        maps.append({
            "x": xt_, "xh": xh_, "cT": cT,
            "wada": np.ascontiguousarray(wada0[:, r * 1536:(r + 1) * 1536]),
            "bada": col(bada0[r * 1536:(r + 1) * 1536], 12),
            "gn": gn, "w_in": w_in0, "w_out": w_out0, "w_gate": wg0, "w_up": wu0, "w_down": wd0,
            "lruc": lruc.reshape(128, 64), "wgate": wgate.reshape(128, 16 * 128), "wpool": wpool,
            "lsp": lsp, "invc": invc.reshape(128, 64), "msk": msk,
        })
    return maps


def kernel(**inputs):
    maps = _host_inputs(**inputs)
    nc = build_nc()
    res = run_bass_kernel_spmd(nc, maps, core_ids=list(range(NCORE)))
    out = np.empty((2, 16384, D), np.float32)
    for r in range(NCORE):
        b, q = r // 4, r % 4
        o = np.asarray(res.results[r]["out"]).reshape(NT, 128, NKC, T)
        out[b, q * 4096:(q + 1) * 4096, :] = o.transpose(0, 3, 2, 1).reshape(4096, D)
    return out
```

```python
from contextlib import ExitStack
import numpy as np
import concourse.bass as bass
import concourse.mybir as mybir
from concourse.bass_utils import run_bass_kernel_spmd

F32 = mybir.dt.float32
BF16 = mybir.dt.bfloat16
AF = mybir.ActivationFunctionType
ALU = mybir.AluOpType

D = 2048
NKC = 16
T = 512
NT = 8
DFF = 5632
NJ = 44
EPS = 1e-6
NCORE = 8


class Trk:
    __slots__ = ("w", "r")

    def __init__(self):
        self.w = None
        self.r = []


class Buf:
    def __init__(self, ap, trks=None):
        self.ap = ap
        self.trks = trks if trks is not None else [Trk()]

    def v(self, ap):
        return Buf(ap, self.trks)


class Sched:
    ENG = ("pe", "act", "dve", "pool", "sp")

    def __init__(self, nc, es):
        self.nc = nc
        self.es = es
        self.q = {e: [] for e in self.ENG}
        self.cnt = {e: 0 for e in self.ENG}
        self.sem = {e: es.enter_context(nc.semaphore("sem_" + e)) for e in self.ENG}
        self.waited = {e: {} for e in self.ENG}
        self.semval = {}

    def new_sem(self, name):
        s = self.es.enter_context(self.nc.semaphore(name))
        self.semval[name] = 0
        return (name, s)

    def _deps(self, eng, reads, writes):
        deps = {}

        def add(d):
            if d is None:
                return
            k, s, v = d
            if k == "pe" and eng == "pe":
                return
            if deps.get(k, (None, 0))[1] < v:
                deps[k] = (s, v)

        for b in reads:
            for t in b.trks:
                add(t.w)
        for b in writes:
            for t in b.trks:
                add(t.w)
                for r in t.r:
                    add(r)
        for k, (s, v) in deps.items():
            if self.waited[eng].get(k, 0) < v:
                self.waited[eng][k] = v
                self.q[eng].append(("wait", s, v))

    def _mark(self, dep, reads, writes):
        for b in reads:
            for t in b.trks:
                t.r.append(dep)
        for b in writes:
            for t in b.trks:
                t.w = dep
                t.r = []

    def op(self, eng, fn, reads=(), writes=()):
        self._deps(eng, reads, writes)
        self.cnt[eng] += 1
        self.q[eng].append(("inst", fn, self.sem[eng], 1))
        self._mark((eng, self.sem[eng], self.cnt[eng]), reads, writes)

    def dma(self, eng, out_ap, in_ap, semh, reads=(), writes=(), serialize=False):
        self.dma_batch(eng, [(out_ap, in_ap, reads, writes)], semh, serialize)

    def dma_batch(self, eng, items, semh, serialize=True):
        name, s = semh
        if serialize and self.semval[name] > 0:
            self.q[eng].append(("wait", s, self.semval[name]))
        total = self.semval[name] + 16 * len(items)
        for out_ap, in_ap, reads, writes in items:
            self._deps(eng, reads, writes)
            self.q[eng].append(("inst", (lambda o, i_: (lambda e: e.dma_start(out=o, in_=i_)))(out_ap, in_ap), s, 16))
        for out_ap, in_ap, reads, writes in items:
            self._mark((name, s, total), reads, writes)
        self.semval[name] = total

    def wait_sem(self, eng, semh):
        name, s = semh
        self.q[eng].append(("wait", s, self.semval[name]))

    def emit(self, eng, e):
        for item in self.q[eng]:
            if item[0] == "wait":
                e.wait_ge(item[1], item[2])
            else:
                _, fn, s, n = item
                fn(e).then_inc(s, n)


def build_nc():
    nc = bass.Bass("TRN2", target_bir_lowering=False)
    dt = nc.dram_tensor
    x_d = dt("x", [NT, 128, NKC * T], F32, kind="ExternalInput").ap()
    xh_d = dt("xh", [128, NKC * 16], F32, kind="ExternalInput").ap()
    cT_d = dt("cT", [128, NKC * 2], F32, kind="ExternalInput").ap()
    wada_d = dt("wada", [D, 1536], F32, kind="ExternalInput").ap()
    bada_d = dt("bada", [128, 12], F32, kind="ExternalInput").ap()
    gn_d = dt("gn", [128, 48], F32, kind="ExternalInput").ap()
    win_d = dt("w_in", [D, 3072], F32, kind="ExternalInput").ap()
    wout_d = dt("w_out", [D, D], F32, kind="ExternalInput").ap()
    wg_d = dt("w_gate", [D, DFF], F32, kind="ExternalInput").ap()
    wu_d = dt("w_up", [D, DFF], F32, kind="ExternalInput").ap()
    wd_d = dt("w_down", [DFF, D], F32, kind="ExternalInput").ap()
    lruc_d = dt("lruc", [128, 64], F32, kind="ExternalInput").ap()
    wgate_d = dt("wgate", [128, 16 * 128], F32, kind="ExternalInput").ap()
    wpool_d = dt("wpool", [128, 8 * 256], F32, kind="ExternalInput").ap()
    lsp_d = dt("lsp", [128, 8], F32, kind="ExternalInput").ap()
    invc_d = dt("invc", [128, 64], F32, kind="ExternalInput").ap()
    msk_d = dt("msk", [128, 12], F32, kind="ExternalInput").ap()
    out_d = dt("out", [NT, 128, NKC * T], F32, kind="ExternalOutput").ap()

    ws_in = dt("ws_in", [6, 128, 16 * 512], BF16).ap()
    ws_out = dt("ws_out", [4, 128, 16 * 512], BF16).ap()
    ws_g = dt("ws_g", [11, 128, 16 * 512], BF16).ap()
    ws_u = dt("ws_u", [11, 128, 16 * 512], BF16).ap()
    ws_d = dt("ws_d", [16, 128, 11 * 512], BF16).ap()
    ag1i = dt("ag1i", [128, 24], F32).ap()
    ag1o = dt("ag1o", [NCORE * 128, 24], F32).ap()
    ag2i = dt("ag2i", [128, 16], F32).ap()
    ag2o = dt("ag2o", [NCORE * 128, 16], F32).ap()

    es = ExitStack()
    with es:
        S = Sched(nc, es)

        def sb(name, shape, dtype):
            return es.enter_context(nc.sbuf_tensor(name, shape, dtype))

        xt = [sb("xt%d" % i, [128, NKC * T], F32) for i in range(2)]
        xb = [[Buf(xt[i][:, k * T:(k + 1) * T]) for k in range(NKC)] for i in range(2)]
        xall = [Buf(xt[i][:], [b.trks[0] for b in xb[i]]) for i in range(2)]
        ht = sb("ht", [128, NKC * T], BF16)
        hb = [Buf(ht[:, k * T:(k + 1) * T]) for k in range(NKC)]
        NSQ = 5
        sqt = [Buf(sb("sq%d" % i, [128, T], BF16)[:]) for i in range(NSQ)]
        tbt = [Buf(sb("tb%d" % i, [128, T], F32)[:]) for i in range(2)]
        rstdA = Buf(sb("rstdA", [128, T], F32)[:])
        rstdB = Buf(sb("rstdB", [128, T], F32)[:])
        wt = [sb("wslot%d" % i, [128, 16 * 512], BF16) for i in range(2)]
        wslot = [Buf(wt[i][:]) for i in range(2)]
        UE = 33536
        ut = sb("U", [128, UE], BF16)
        PG = 512
        upages = [Trk() for _ in range((UE + PG - 1) // PG)]

        def ubuf(off, n, dtype=BF16):
            ap = ut[:, off:off + n]
            if dtype == F32:
                ap = ap.bitcast(F32)
            return Buf(ap, upages[off // PG:(off + n - 1) // PG + 1])

        act = [ubuf(j * T, T) for j in range(NJ)]
        sgt = [ubuf(NJ * T + i * 2 * T, 2 * T, F32) for i in range(4)]
        NSET = 7
        o = 0
        LS = []
        for s in range(NSET):
            d_ = {}
            d_["XR"] = ubuf(o, 2 * 516, F32); o += 2 * 516
            d_["XCB"] = ubuf(o, T); o += T
            d_["TI"] = ubuf(o, 2 * T, F32); o += 2 * T
            d_["XC"] = ubuf(o, 2 * T, F32); o += 2 * T
            d_["M"] = ubuf(o, 2 * T, F32); o += 2 * T
            d_["PM"] = ubuf(o - 4 * T, 4 * T, F32)
            LS.append(d_)
        assert o <= UE, o
        o = 0
        ybuf = []
        for k in range(16):
            ybuf.append(ubuf(o, T)); o += T
        XP = []
        for s in range(2):
            XP.append(ubuf(o, 2 * 528, F32)); o += 2 * 528
        PA = ubuf(o, 2 * 528, F32); o += 2 * 528
        PB_ = ubuf(o, 2 * 528, F32); o += 2 * 528
        PBF = []
        for s in range(4):
            PBF.append(ubuf(o, T)); o += T
        Gt = []
        for s in range(2):
            Gt.append(ubuf(o, 2 * T, F32)); o += 2 * T
        LPG = []
        for s in range(2):
            LPG.append(ubuf(o, 16 * T, F32)); o += 16 * T
        assert o <= UE, o
        wada_st = ubuf(0, 2 * 16 * 768, F32)

        cst = sb("cst", [128, 1056], F32)
        co = [0]

        def cbuf(n):
            b = Buf(cst[:, co[0]:co[0] + n])
            co[0] += n
            return b

        cTb = cbuf(32); siluT = cbuf(32); badab = cbuf(12); gnb = cbuf(48)
        lrucb = cbuf(64); lspb = cbuf(8); invcb = cbuf(64); mskb = cbuf(12)
        modpart = cbuf(24); modall = cbuf(192); modsel = cbuf(96); modtmp = cbuf(96)
        geff = cbuf(32); hcl = cbuf(8); hba = cbuf(8); hbi = cbuf(8); ltmp = cbuf(8)
        state = cbuf(8); pstate = cbuf(8); ag2sb = cbuf(128); xrh = cbuf(24); xph = cbuf(120)
        cA = cbuf(8); cH = cbuf(8)
        assert co[0] <= 1056, co[0]
        zeros = Buf(sb("zeros", [128, T], F32)[:])
        ones_bf = Buf(sb("ones", [128, 128], BF16)[:])
        wgb = Buf(sb("wgb", [128, 16 * 128], BF16)[:])
        wpb = Buf(sb("wpb", [128, 8 * 256], BF16)[:])
        xhb = Buf(sb("xhb", [128, NKC * 16], F32)[:])
        hhb = Buf(sb("hhb", [128, NKC * 16], BF16)[:])

        banks = [Buf(es.enter_context(nc.psum_tensor("ps%d" % i, [128, T], F32))[:]) for i in range(8)]
        bk = [0]

        def bank():
            b = banks[bk[0] % 6]
            bk[0] += 1
            return b

        s_misc = S.new_sem("s_misc")
        s_const = s_misc
        s_xl = S.new_sem("s_x")
        s_x = [s_xl, s_xl]
        s_ol = S.new_sem("s_o")
        s_o = [s_ol, s_ol]
        s_w = [S.new_sem("s_w0"), S.new_sem("s_w1")]
        s_cv = [S.new_sem("s_cv0"), S.new_sem("s_cv1")]
        s_ph = S.new_sem("s_ph")
        s_lpq = S.new_sem("s_lpq")

        def A(eng, fn, reads, writes):
            S.op(eng, fn, reads, writes)

        def act_fn(out, in_, func, reads, writes, bias=None, scale=None):
            kw = {}
            if bias is not None:
                kw["bias"] = bias
            if scale is not None:
                kw["scale"] = scale
            A("act", lambda e: e.activation(out=out, in_=in_, func=func, **kw), reads, writes)

        def tt(out, in0, in1, op, reads, writes):
            A("dve", lambda e: e.tensor_tensor(out=out, in0=in0, in1=in1, op=op), reads, writes)

        def ts(out, in0, s1, s2, op0, op1, reads, writes):
            if s2 is None:
                A("dve", lambda e: e.tensor_scalar(out=out, in0=in0, scalar1=s1, scalar2=None, op0=op0), reads, writes)
            else:
                A("dve", lambda e: e.tensor_scalar(out=out, in0=in0, scalar1=s1, scalar2=s2, op0=op0, op1=op1), reads, writes)

        def stt(out, in0, sc, in1, op0, op1, reads, writes):
            A("dve", lambda e: e.scalar_tensor_tensor(out=out, in0=in0, scalar=sc, in1=in1, op0=op0, op1=op1), reads, writes)

        def cp(out, in_, reads, writes):
            A("dve", lambda e: e.tensor_copy(out=out, in_=in_), reads, writes)

        def mm_group(outb, pairs, reads, start=True, stop=True, out_ap=None, fresh_check=True):
            oap = out_ap if out_ap is not None else outb.ap
            n = len(pairs)
            if start and fresh_check:
                for t in outb.trks:
                    assert t.w is None or t.w[0] != "pe" or len(t.r) > 0, "PSUM bank reused before it was read"

            def fn(e):
                last = None
                for i, (l, r) in enumerate(pairs):
                    last = e.matmul(oap, l, r, start=(start and i == 0), stop=(stop and i == n - 1))
                return last
            A("pe", fn, reads, [outb])

        cl = [(cTb, cT_d), (badab, bada_d), (gnb, gn_d), (lrucb, lruc_d), (lspb, lsp_d),
              (invcb, invc_d), (mskb, msk_d), (xhb, xh_d)]
        stg1 = Buf(xt[1][:, 0:2048], [t for b in xb[1][0:4] for t in b.trks])
        stg2 = Buf(xt[1][:, 2048:4096], [t for b in xb[1][4:8] for t in b.trks])
        items = [(b.ap, d_[:, :], [], [b]) for b, d_ in cl]
        items.append((stg1.ap, wgate_d[:, :], [], [stg1]))
        items.append((stg2.ap, wpool_d[:, :], [], [stg2]))
        S.dma_batch("sp", items, s_const)
        cp(wgb.ap, stg1.ap, [stg1], [wgb])
        cp(wpb.ap, stg2.ap, [stg2], [wpb])
        A("dve", lambda e: e.memset(zeros.ap, 0.0), [], [zeros])
        A("dve", lambda e: e.memset(ones_bf.ap, 1.0), [], [ones_bf])
        A("dve", lambda e: e.memset(state.ap, 0.0), [], [state])
        A("dve", lambda e: e.memset(pstate.ap, 1.0), [], [pstate])

        lc3 = lrucb.ap.rearrange("p (c k) -> p c k", k=8)
        act_fn(ltmp.ap, lc3[:, :, 7], AF.Exp, [lrucb], [ltmp], scale=-1.0)
        act_fn(ltmp.ap, ltmp.ap, AF.Ln, [ltmp], [ltmp], bias=1.0)
        ts(hcl.ap, ltmp.ap, -4.0, None, ALU.mult, None, [ltmp], [hcl])
        ts(hba.ap, lc3[:, :, 5], 0.5, None, ALU.mult, None, [lrucb], [hba])
        ts(hbi.ap, lc3[:, :, 6], 0.5, None, ALU.mult, None, [lrucb], [hbi])

        act_fn(siluT.ap, cTb.ap, AF.Silu, [cTb], [siluT])
        pm = bank()
        for half in range(2):
            S.dma("sp", wada_st.ap.rearrange("p (k j) -> p k j", k=16),
                  wada_d[:, half * 768:(half + 1) * 768].rearrange("(k p) j -> p k j", p=128),
                  s_misc, [], [wada_st], serialize=True)
            w3 = wada_st.ap.rearrange("p (k j) -> p k j", k=16)
            for jj in range(6):
                col = (half * 6 + jj) * 2
                pairs = [(w3[:, k, jj * 128:(jj + 1) * 128], siluT.ap[:, 2 * k:2 * k + 2]) for k in range(16)]
                mm_group(pm, pairs, [wada_st, siluT], out_ap=pm.ap[:, col:col + 2], fresh_check=False)
        pm3 = pm.ap[:, 0:24].rearrange("p (j b) -> p j b", b=2)
        mp3 = modpart.ap.rearrange("p (j b) -> p j b", b=2)
        for b_ in range(2):
            tt(mp3[:, :, b_], pm3[:, :, b_], badab.ap, ALU.add, [pm, badab], [modpart])
        ag1i_b = Buf(ag1i); ag1o_b = Buf(ag1o); ag2i_b = Buf(ag2i); ag2o_b = Buf(ag2o)
        S.dma("sp", ag1i[:, :], modpart.ap, s_misc, [modpart], [ag1i_b], serialize=True)

        conv_list = []
        scr_in = {}; scr_out = {}; scr_g = {}; scr_u = {}; scr_d = {}
        for s in (0, 1, 4, 5, 2, 3):
            conv_list.append((ws_in[s], win_d[:, s * 512:(s + 1) * 512], 16, scr_in, s))
        for s in range(4):
            conv_list.append((ws_out[s], wout_d[:, s * 512:(s + 1) * 512], 16, scr_out, s))
        for s in range(11):
            conv_list.append((ws_g[s], wg_d[:, s * 512:(s + 1) * 512], 16, scr_g, s))
            conv_list.append((ws_u[s], wu_d[:, s * 512:(s + 1) * 512], 16, scr_u, s))
        for mg in range(4):
            for kg in range(4):
                conv_list.append((ws_d[mg * 4 + kg],
                                  wd_d[kg * 11 * 128:(kg + 1) * 11 * 128, mg * 512:(mg + 1) * 512], 11,
                                  scr_d, mg * 4 + kg))
        for dst, src, nk, dct, key in conv_list:
            dct[key] = Buf(dst)
        cvi = [0]

        def do_conv(n):
            for _ in range(n):
                if cvi[0] >= len(conv_list):
                    return
                i = cvi[0]
                dst, src, nk, dct, key = conv_list[i]
                b = dct[key]
                semh = s_cv[i % 2]
                if i >= 1:
                    prev = s_cv[(i - 1) % 2]
                    S.q["pool"].append(("wait", prev[1], S.semval[prev[0]]))
                S.dma("pool", dst.rearrange("p (k j) -> p k j", k=nk),
                      src.rearrange("(k p) j -> p k j", p=128), semh, [], [b])
                cvi[0] += 1

        do_conv(4)
        A("pool", lambda e: e.collective_compute("AllGather", ALU.bypass, replica_groups=[list(range(NCORE))],
                                                 ins=[ag1i[:, :]], outs=[ag1o[:, :]]), [ag1i_b], [ag1o_b])
        S.dma("sp", modall.ap.rearrange("p (r c) -> p r c", r=NCORE),
              ag1o.rearrange("(r p) c -> p r c", p=128), s_misc, [ag1o_b], [modall], serialize=True)
        ma3 = modall.ap.rearrange("p (n b) -> p n b", b=2)
        ts(modtmp.ap, ma3[:, :, 0], mskb.ap[:, 9:10], None, ALU.mult, None, [modall, mskb], [modtmp])
        stt(modsel.ap, ma3[:, :, 1], mskb.ap[:, 10:11], modtmp.ap, ALU.mult, ALU.add, [modall, mskb, modtmp], [modsel])
        SH1, SC1, GT1, SH2, SC2, GT2 = [modsel.ap[:, i * 16:(i + 1) * 16] for i in range(6)]
        stt(geff.ap[:, 0:16], SC1, 1.0, gnb.ap[:, 0:16], ALU.add, ALU.mult, [modsel, gnb], [geff])
        stt(geff.ap[:, 16:32], SC2, 1.0, gnb.ap[:, 16:32], ALU.add, ALU.mult, [modsel, gnb], [geff])
        do_conv(16)

        def load_x(i, par):
            S.dma("sp", xt[par][:], x_d[i], s_x[par], [], [xall[par]], serialize=True)

        class NormAcc:
            def __init__(self, bankb, bufs, width=T, lag=0):
                assert lag < len(bufs)
                self.b = bankb
                self.bufs = bufs
                self.n = 0
                self.k = 0
                self.w = width
                self.lag = lag
                self.pend = []

            def _mm(self):
                w = self.w
                sq = self.pend.pop(0)
                mm_group(self.b, [(ones_bf.ap, sq.ap[:, :w])], [ones_bf, sq],
                         start=(self.k == 0), stop=(self.k == NKC - 1), out_ap=self.b.ap[:, :w])
                self.k += 1

            def add(self, xk):
                w = self.w
                assert len(self.pend) < len(self.bufs)
                sq = self.bufs[self.n % len(self.bufs)]
                self.n += 1
                xin = xk.ap[:, :w] if w == T else xk.ap
                act_fn(sq.ap[:, :w], xin, AF.Square, [xk], [sq])
                self.pend.append(sq)
                while len(self.pend) > self.lag:
                    self._mm()

            def finish(self, rs):
                w = self.w
                while self.pend:
                    self._mm()
                assert self.k == NKC
                act_fn(rs.ap[:, :w], self.b.ap[:, :w], AF.Sqrt, [self.b], [rs], bias=EPS, scale=1.0 / D)
                A("dve", lambda e: e.reciprocal(out=rs.ap[:, :w], in_=rs.ap[:, :w]), [rs], [rs])

        def norm_stats(xs, rs, bankb, width=T):
            na = NormAcc(bankb, sqt[3:5], width, lag=1)
            for k in range(NKC):
                na.add(xs[k])
            na.finish(rs)

        def norm_mod(xs, gcol, shap, hs, rs, width=T):
            for k in range(NKC):
                tb = tbt[k % 2]
                xin = xs[k].ap[:, :width] if width == T else xs[k].ap
                tt(tb.ap[:, :width], xin, rs.ap[:, :width], ALU.mult, [xs[k], rs], [tb])
                act_fn(hs[k].ap if width != T else hs[k].ap[:, :width], tb.ap[:, :width], AF.Identity,
                       [tb, geff, modsel], [hs[k]], bias=shap[:, k:k + 1], scale=geff.ap[:, gcol + k:gcol + k + 1])

        wq = {"n": 0}

        def load_slab(src_buf, nk=16):
            i = wq["n"] % 2
            wq["n"] += 1
            S.dma("sp", wt[i][:, 0:nk * 512], src_buf.ap, s_w[i], [src_buf], [wslot[i]])
            return wslot[i], wt[i][:, 0:nk * 512].rearrange("p (k j) -> p k j", k=nk)

        def proj_chunk(slot, w3, jj, rhs_list, nk=16):
            ps = bank()
            pairs = [(w3[:, k, jj * 128:(jj + 1) * 128], rhs_list[k].ap) for k in range(nk)]
            mm_group(ps, pairs, [slot] + list(rhs_list))
            return ps

        wc = lambda c, k: lrucb.ap[:, c * 8 + k:c * 8 + k + 1]

        outB = [Buf(out_d[i]) for i in range(NT)]
        lpB = [[outB[i].v(out_d[i][:, c * 2 * T:(c + 1) * 2 * T]) for c in range(8)] for i in range(NT)]

        def lru_conv(c, ps, st):
            XR, XC, XCB = st["XR"], st["XC"], st["XCB"]
            cp(XR.ap[:, 0:3], xrh.ap[:, c * 3:c * 3 + 3], [xrh], [XR])
            act_fn(XR.ap[:, 3:515], ps.ap, AF.Copy, [ps], [XR])
            cp(xrh.ap[:, c * 3:c * 3 + 3], XR.ap[:, 512:515], [XR], [xrh])
            act_fn(XC.ap, XR.ap[:, 3:515], AF.Identity, [XR, lrucb], [XC], bias=wc(c, 4), scale=wc(c, 3))
            for k in (2, 1, 0):
                stt(XC.ap, XR.ap[:, k:k + 512], wc(c, k), XC.ap, ALU.mult, ALU.add, [XR, XC, lrucb], [XC])
            act_fn(XCB.ap, XC.ap, AF.Copy, [XC], [XCB])

        def lru_gates(c, st):
            pa = bank()
            mm_group(pa, [(wgb.ap[:, c * 128:(c + 1) * 128], st["XCB"].ap)], [wgb, st["XCB"]])
            pi = bank()
            mm_group(pi, [(wgb.ap[:, (8 + c) * 128:(9 + c) * 128], st["XCB"].ap)], [wgb, st["XCB"]])
            TA = st["XR"].ap[:, 0:512]
            act_fn(TA, pa.ap, AF.Tanh, [pa, hba], [st["XR"]], bias=hba.ap[:, c:c + 1], scale=0.5)
            act_fn(st["TI"].ap, pi.ap, AF.Tanh, [pi, hbi], [st["TI"]], bias=hbi.ap[:, c:c + 1], scale=0.5)

        def lru_exp(c, st):
            TA = st["XR"].ap[:, 0:512]
            act_fn(TA, TA, AF.Exp, [st["XR"], hcl], [st["XR"]], bias=hcl.ap[:, c:c + 1], scale=hcl.ap[:, c:c + 1])
            act_fn(st["M"].ap, TA, AF.Square, [st["XR"]], [st["M"]])
            stt(st["TI"].ap, st["TI"].ap, 1.0, st["XC"].ap, ALU.add, ALU.mult, [st["TI"], st["XC"]], [st["TI"]])

        def lru_sqrt_scan(i, c, st, sidx):
            TA = st["XR"].ap[:, 0:512]
            act_fn(st["M"].ap, st["M"].ap, AF.Sqrt, [st["M"]], [st["M"]], bias=0.25, scale=-0.25)
            tt(st["TI"].ap, st["TI"].ap, st["M"].ap, ALU.mult, [st["TI"], st["M"]], [st["TI"]])
            A("dve", lambda e: e.tensor_tensor_scan(out=st["M"].ap, data0=TA, data1=st["TI"].ap,
                                                    initial=state.ap[:, c:c + 1], op0=ALU.mult, op1=ALU.add),
              [st["XR"], st["TI"], state], [st["M"]])
            cp(state.ap[:, c:c + 1], st["M"].ap[:, 511:512], [st["M"]], [state])
            A("dve", lambda e: e.tensor_tensor_scan(out=st["XC"].ap, data0=TA, data1=zeros.ap,
                                                    initial=pstate.ap[:, c:c + 1], op0=ALU.mult, op1=ALU.add),
              [st["XR"], zeros, pstate], [st["XC"]])
            cp(pstate.ap[:, c:c + 1], st["XC"].ap[:, 511:512], [st["XC"]], [pstate])

        setc = [0]
        grpc = [0]

        def lru_group(i, cs, slot_x, w3x, hs):
            sid = {}
            for c in cs:
                sid[c] = setc[0] % NSET
                setc[0] += 1
            sts = {c: LS[sid[c]] for c in cs}
            for c in cs:
                ps = proj_chunk(slot_x, w3x, c % 4, hs)
                lru_conv(c, ps, sts[c])
            for c in cs:
                lru_gates(c, sts[c])
            for c in cs:
                lru_exp(c, sts[c])
            for c in cs:
                lru_sqrt_scan(i, c, sts[c], sid[c])
            items = []
            for c in cs:
                items.append((lpB[i][c].ap, sts[c]["PM"].ap, [sts[c]["PM"]], [lpB[i][c]]))
            S.dma_batch("sp", items, s_ph)

        def pool_chunk(c, ps, first_tile):
            g = c // 2
            w = 2 << g
            X = XP[c % 2]
            cp(X.ap[:, 0:15], xph.ap[:, c * 15:c * 15 + 15], [xph], [X])
            act_fn(X.ap[:, 15:527], ps.ap, AF.Copy, [ps], [X])
            cp(xph.ap[:, c * 15:c * 15 + 15], X.ap[:, 512:527], [X], [xph])
            src = X
            off = 0
            dsts = [PA, PB_]
            for lv in range(g + 1):
                sh = 1 << lv
                dst = dsts[lv % 2]
                n0 = off + sh
                tt(dst.ap[:, n0:527], src.ap[:, n0:527], src.ap[:, n0 - sh:527 - sh], ALU.add, [src], [dst])
                src = dst
                off = n0
            pb = PBF[c % 4]
            stt(pb.ap, src.ap[:, 15:527], 1.0 / w, X.ap[:, 15:527], ALU.mult, ALU.subtract, [src, X], [pb])
            if first_tile:
                t16 = tbt[0]
                tt(t16.ap[:, 0:16], src.ap[:, 15:31], invcb.ap[:, g * 16:(g + 1) * 16], ALU.mult, [src, invcb], [t16])
                tt(pb.ap[:, 0:16], t16.ap[:, 0:16], X.ap[:, 15:31], ALU.subtract, [t16, X], [pb])
            return pb

        def resid_update(xs, m, ps, gcol):
            stt(xs[m].ap, ps.ap, modsel.ap[:, gcol + m:gcol + m + 1], xs[m].ap, ALU.mult, ALU.add,
                [ps, modsel, xs[m]], [xs[m]])

        def halo_proj():
            hx = [Buf(xhb.ap[:, k * 16:(k + 1) * 16], xhb.trks) for k in range(NKC)]
            hh = [Buf(hhb.ap[:, k * 16:(k + 1) * 16], hhb.trks) for k in range(NKC)]
            norm_stats(hx, rstdA, banks[6], width=16)
            norm_mod(hx, 0, SH1, hh, rstdA, width=16)
            res = {}
            for s in (4, 5, 0, 1):
                slot, w3 = load_slab(scr_in[s])
                res[s] = (slot, w3)
                for jj in range(4):
                    ps = bank()
                    pairs = [(w3[:, k, jj * 128:(jj + 1) * 128], hh[k].ap) for k in range(NKC)]
                    mm_group(ps, pairs, [slot, hhb], out_ap=ps.ap[:, 0:16])
                    c = s * 4 + jj if s < 2 else (s - 4) * 4 + jj
                    if s < 2:
                        ts(xrh.ap[:, c * 3:c * 3 + 3], ps.ap[:, 13:16], mskb.ap[:, 8:9], None, ALU.mult, None,
                           [ps, mskb], [xrh])
                    else:
                        ts(xph.ap[:, c * 15:c * 15 + 15], ps.ap[:, 1:16], mskb.ap[:, 8:9], None, ALU.mult, None,
                           [ps, mskb], [xph])
            return res

        load_x(0, 0)
        hres = halo_proj()
        slot0, w30 = hres[0]
        slot1, w31 = hres[1]
        for i in range(NT):
            par = i % 2
            if i + 1 < NT:
                load_x(i + 1, (i + 1) % 2)
            norm_stats(xb[par], rstdA, banks[6])
            norm_mod(xb[par], 0, SH1, hb, rstdA)
            lru_group(i, [0, 1, 2, 3], slot0, w30, hb)
            lru_group(i, [4, 5, 6, 7], slot1, w31, hb)
        load_x(0, 0)
        cp(ag2sb.ap[:, 0:8], pstate.ap, [pstate], [ag2sb])
        cp(ag2sb.ap[:, 8:16], state.ap, [state], [ag2sb])
        S.dma("sp", ag2i[:, :], ag2sb.ap[:, 0:16], s_misc, [ag2sb], [ag2i_b], serialize=True)
        A("pool", lambda e: e.collective_compute("AllGather", ALU.bypass, replica_groups=[list(range(NCORE))],
                                                 ins=[ag2i[:, :]], outs=[ag2o[:, :]]), [ag2i_b], [ag2o_b])

        def combine_state():
            S.dma("sp", ag2sb.ap.rearrange("p (r c) -> p r c", r=NCORE),
                  ag2o.rearrange("(r p) c -> p r c", p=128), s_misc, [ag2o_b], [ag2sb], serialize=True)
            A("dve", lambda e: e.memset(state.ap, 0.0), [], [state])
            for r in range(NCORE):
                mj = mskb.ap[:, r:r + 1]
                Ar = ag2sb.ap[:, r * 16:r * 16 + 8]
                Hr = ag2sb.ap[:, r * 16 + 8:r * 16 + 16]
                ts(cA.ap, Ar, -1.0, None, ALU.add, None, [ag2sb], [cA])
                ts(cA.ap, cA.ap, mj, None, ALU.mult, None, [cA, mskb], [cA])
                ts(cA.ap, cA.ap, 1.0, None, ALU.add, None, [cA], [cA])
                ts(cH.ap, Hr, mj, None, ALU.mult, None, [ag2sb, mskb], [cH])
                tt(state.ap, state.ap, cA.ap, ALU.mult, [state, cA], [state])
                tt(state.ap, state.ap, cH.ap, ALU.add, [state, cH], [state])

        norm_stats(xb[0], rstdA, banks[6])
        norm_mod(xb[0], 0, SH1, hb, rstdA)
        for i in range(NT):
            par = i % 2
            npar = (i + 1) % 2
            xs = xb[par]
            first = (i == 0)
            for g in range(2):
                S.dma("pool", LPG[g].ap, out_d[i][:, g * 8 * T:(g + 1) * 8 * T], s_lpq, [outB[i]], [LPG[g]],
                      serialize=True)
            if first:
                do_conv(100)
            for s in (2, 3):
                sg_, wg_ = load_slab(scr_in[s])
                for jj in range(4):
                    c = (s - 2) * 4 + jj
                    lp = LPG[s - 2].v(LPG[s - 2].ap[:, jj * 2 * T:(jj + 1) * 2 * T])
                    if first and c == 0:
                        combine_state()
                    pg = proj_chunk(sg_, wg_, jj, hb)
                    G = Gt[c % 2]
                    act_fn(G.ap, pg.ap, AF.Gelu, [pg], [G])
                    stt(lp.ap[:, 0:T], lp.ap[:, 0:T], state.ap[:, c:c + 1], lp.ap[:, T:2 * T],
                        ALU.mult, ALU.add, [lp, state], [lp])
                    tt(ybuf[c].ap, lp.ap[:, 0:T], G.ap, ALU.mult, [lp, G], [ybuf[c]])
            for s in (4, 5):
                sp_, wp_ = load_slab(scr_in[s])
                for jj in range(4):
                    c = (s - 4) * 4 + jj
                    ps = proj_chunk(sp_, wp_, jj, hb)
                    pool_chunk(c, ps, first)
                    if c % 2 == 1:
                        g = c // 2
                        for dd in range(2):
                            po = bank()
                            pairs = [(wpb.ap[:, (g * 2 + kc) * 256 + dd * 128:(g * 2 + kc) * 256 + (dd + 1) * 128],
                                      PBF[(2 * g + kc) % 4].ap) for kc in range(2)]
                            mm_group(po, pairs, [wpb, PBF[(2 * g) % 4], PBF[(2 * g + 1) % 4]])
                            yb = ybuf[8 + 2 * g + dd]
                            act_fn(yb.ap, po.ap, AF.Identity, [po, lspb], [yb],
                                   scale=lspb.ap[:, 2 * g + dd:2 * g + dd + 1])
            na = NormAcc(banks[6], sqt[3:5], lag=1)
            for s in range(4):
                so, wo = load_slab(scr_out[s])
                for jj in range(4):
                    ps = proj_chunk(so, wo, jj, ybuf)
                    resid_update(xs, s * 4 + jj, ps, 32)
                    na.add(xs[s * 4 + jj])
            if i + 1 < NT:
                load_x(i + 1, npar)
            na.finish(rstdA)
            norm_mod(xs, 16, SH2, hb, rstdA)
            for s in range(11):
                sgs, wgs = load_slab(scr_g[s])
                sus, wus = load_slab(scr_u[s])
                for jj in range(4):
                    pg = proj_chunk(sgs, wgs, jj, hb)
                    act_fn(sgt[jj].ap, pg.ap, AF.Silu, [pg], [sgt[jj]])
                for jj in range(4):
                    j = s * 4 + jj
                    pu = proj_chunk(sus, wus, jj, hb)
                    tt(act[j].ap, sgt[jj].ap, pu.ap, ALU.mult, [sgt[jj], pu], [act[j]])
            nf = NormAcc(banks[7], sqt[0:3], lag=2)
            n1 = None
            for mg in range(4):
                pbs = [bank() for _ in range(4)]
                for kg in range(4):
                    sd, wd3 = load_slab(scr_d[mg * 4 + kg], nk=11)
                    for m in range(4):
                        pairs = [(wd3[:, k, m * 128:(m + 1) * 128], act[kg * 11 + k].ap) for k in range(11)]
                        mm_group(pbs[m], pairs, [sd] + act[kg * 11:(kg + 1) * 11], start=(kg == 0), stop=(kg == 3))
                        if mg == 1 and i + 1 < NT:
                            if n1 is None:
                                n1 = NormAcc(banks[6], sqt[3:5], lag=1)
                            n1.add(xb[npar][kg * 4 + m])
                for m in range(4):
                    resid_update(xs, mg * 4 + m, pbs[m], 80)
                    nf.add(xs[mg * 4 + m])
                if mg == 1 and i + 1 < NT:
                    n1.finish(rstdA)
                    norm_mod(xb[npar], 0, SH1, hb, rstdA)
            nf.finish(rstdB)
            for k in range(NKC):
                stt(xs[k].ap, xs[k].ap, gnb.ap[:, 32 + k:33 + k], rstdB.ap, ALU.mult, ALU.mult,
                    [xs[k], gnb, rstdB], [xs[k]])
            S.dma("sp", out_d[i], xt[par][:], s_o[par], [xall[par]], [outB[i]], serialize=True)
        for par in range(2):
            S.wait_sem("sp", s_o[par])

        with nc.Block() as block:
            @block.sync
            def _(e):
                S.emit("sp", e)

            @block.gpsimd
            def _(e):
                S.emit("pool", e)

            @block.scalar
            def _(e):
                S.emit("act", e)

            @block.vector
            def _(e):
                S.emit("dve", e)

            @block.tensor
            def _(e):
                S.emit("pe", e)
    return nc


def _host_inputs(x, c, w_ada, b_ada, g_norm_mix, w_in, w_conv, b_conv, w_rg_a, b_rg_a,
                 w_rg_i, b_rg_i, lru_lambda, w_pool, ls_pool, w_out, g_norm_ffn,
                 w_ffn_gate, w_ffn_up, w_ffn_down, g_norm_final):
    f = np.float32
    x = np.asarray(x, f); c = np.asarray(c, f)
    col = lambda v, n: np.ascontiguousarray(np.asarray(v, f).reshape(n, 128).T)
    cT = np.ascontiguousarray(c.reshape(2, 16, 128).transpose(2, 1, 0)).reshape(128, 32)
    gn = np.concatenate([col(g_norm_mix[0], 16), col(g_norm_ffn[0], 16), col(g_norm_final, 16)], axis=1)
    lruc = np.zeros((128, 8, 8), f)
    wcv = np.asarray(w_conv[0], f)
    for k in range(4):
        lruc[:, :, k] = col(wcv[k], 8)
    lruc[:, :, 4] = col(b_conv[0], 8)
    lruc[:, :, 5] = col(b_rg_a[0], 8)
    lruc[:, :, 6] = col(b_rg_i[0], 8)
    lruc[:, :, 7] = col(lru_lambda[0], 8)
    wgate = np.zeros((128, 16, 128), f)
    wa = np.asarray(w_rg_a[0], f); wi = np.asarray(w_rg_i[0], f)
    for cc in range(8):
        for hh in range(2):
            wgate[hh * 64:(hh + 1) * 64, cc, hh * 64:(hh + 1) * 64] = wa[2 * cc + hh]
            wgate[hh * 64:(hh + 1) * 64, 8 + cc, hh * 64:(hh + 1) * 64] = wi[2 * cc + hh]
    wp = np.asarray(w_pool[0], f)
    wpool = np.ascontiguousarray(wp.reshape(4, 2, 128, 256).transpose(2, 0, 1, 3)).reshape(128, 8 * 256)
    lsp = col(np.asarray(ls_pool[0], f).reshape(-1), 8)
    w_in0 = np.ascontiguousarray(np.asarray(w_in[0], f))
    w_out0 = np.ascontiguousarray(np.asarray(w_out[0], f))
    wg0 = np.ascontiguousarray(np.asarray(w_ffn_gate[0], f))
    wu0 = np.ascontiguousarray(np.asarray(w_ffn_up[0], f))
    wd0 = np.ascontiguousarray(np.asarray(w_ffn_down[0], f))
    wada0 = np.asarray(w_ada[0], f)
    bada0 = np.asarray(b_ada[0], f)
    maps = []
    for r in range(NCORE):
        b, q = r // 4, r % 4
        t0 = q * 4096
        xs = x[b, t0:t0 + 4096, :]
        xt_ = np.ascontiguousarray(xs.reshape(NT, T, NKC, 128).transpose(0, 3, 2, 1)).reshape(NT, 128, NKC * T)
        if q > 0:
            xh = x[b, t0 - 16:t0, :]
        else:
            xh = np.zeros((16, D), f)
        xh_ = np.ascontiguousarray(xh.reshape(16, NKC, 128).transpose(2, 1, 0)).reshape(128, NKC * 16)
        invc = np.zeros((128, 4, 16), f)
        for g in range(4):
            w = 2 << g
            pos1 = np.arange(t0 + 1, t0 + 17)
            invc[:, g, :] = (1.0 / np.minimum(pos1, w)).astype(f)[None, :]
        msk = np.zeros((128, 12), f)
        for j in range(NCORE):
            if j // 4 == b and j < r:
                msk[:, j] = 1.0
        msk[:, 8] = 1.0 if q > 0 else 0.0
        msk[:, 9] = 1.0 if b == 0 else 0.0
        msk[:, 10] = 1.0 if b == 1 else 0.0
        maps.append({
            "x": xt_, "xh": xh_, "cT": cT,
            "wada": np.ascontiguousarray(wada0[:, r * 1536:(r + 1) * 1536]),
            "bada": col(bada0[r * 1536:(r + 1) * 1536], 12),
            "gn": gn, "w_in": w_in0, "w_out": w_out0, "w_gate": wg0, "w_up": wu0, "w_down": wd0,
            "lruc": lruc.reshape(128, 64), "wgate": wgate.reshape(128, 16 * 128), "wpool": wpool,
            "lsp": lsp, "invc": invc.reshape(128, 64), "msk": msk,
        })
    return maps


def kernel(**inputs):
    maps = _host_inputs(**inputs)
    nc = build_nc()
    res = run_bass_kernel_spmd(nc, maps, core_ids=list(range(NCORE)))
    out = np.empty((2, 16384, D), np.float32)
    for r in range(NCORE):
        b, q = r // 4, r % 4
        o = np.asarray(res.results[r]["out"]).reshape(NT, 128, NKC, T)
        out[b, q * 4096:(q + 1) * 4096, :] = o.transpose(0, 3, 2, 1).reshape(4096, D)
    return out
```
